# Optimizing a Trainium2 kernel written in Bass

```python
import math
import jax, jax.numpy as jnp
from jax import lax
import numpy as np

D_MODEL = 1024
BATCH = 4
SEQ = 8192
DEPTH = 1

N_META = 16
BLOCK = 128
WINDOW = 128
N_Q_HEADS = 16
N_KV_HEADS = 4
HEAD_DIM = 64
Q_GROUP = N_Q_HEADS // N_KV_HEADS
Q_WIDTH = N_Q_HEADS * HEAD_DIM
KV_WIDTH = N_KV_HEADS * HEAD_DIM
POOL_WINDOWS = (2, 4, 8, 16)
N_POOL_GROUPS = 4
POOL_WIDTH = D_MODEL
POOL_GROUP = POOL_WIDTH // N_POOL_GROUPS
N_BRANCHES = 2
IN_WIDTH = POOL_WIDTH + Q_WIDTH + 2 * KV_WIDTH + N_BRANCHES * D_MODEL
D_FF = 4 * D_MODEL
N_BUCKETS = 32
MAX_DISTANCE = 128
RMS_EPS = 1e-5

kernel_name = "hybrid_pool_swa_gated_block"


def rms_norm(x, g):
    xf = x.astype(jnp.float32)
    y = xf * lax.rsqrt(jnp.mean(xf * xf, axis=-1, keepdims=True) + RMS_EPS)
    return (y * g.astype(jnp.float32)).astype(x.dtype)


def t5_causal_bucket(dist):
    n = jnp.maximum(dist, 0)
    max_exact = N_BUCKETS // 2
    nf = jnp.maximum(n, 1).astype(jnp.float32)
    large = max_exact + (jnp.log(nf / max_exact) / math.log(MAX_DISTANCE / max_exact)
                         * (N_BUCKETS - max_exact)).astype(jnp.int32)
    large = jnp.minimum(large, N_BUCKETS - 1)
    return jnp.where(n < max_exact, n, large)


def multiscale_pool(z):
    B, L, _ = z.shape
    zf = z.astype(jnp.float32).reshape(B, L, N_POOL_GROUPS, POOL_GROUP)
    csum = jnp.pad(jnp.cumsum(zf, axis=1), ((0, 0), (1, 0), (0, 0), (0, 0)))
    t = jnp.arange(L)
    outs = []
    for g, w in enumerate(POOL_WINDOWS):
        end = csum[:, 1:, g]
        prev = jnp.pad(csum[:, :L + 1 - w, g], ((0, 0), (w - 1, 0), (0, 0)))
        cnt = jnp.minimum(t + 1, w).astype(jnp.float32)
        outs.append((end - prev) / cnt[None, :, None])
    pooled = jnp.stack(outs, axis=2)
    return (pooled - zf).astype(z.dtype)


def sliding_window_attention(q, k, v, sinks, rel_bias):
    B, L = q.shape[:2]
    pad = BLOCK - N_META
    padf = lambda a: jnp.pad(a, ((0, 0), (pad, 0), (0, 0), (0, 0)))
    qp, kp, vp = padf(q), padf(k), padf(v)
    NB = (L + pad) // BLOCK
    NK = N_META + 2 * BLOCK
    qb = qp.reshape(B, NB, BLOCK, N_KV_HEADS, Q_GROUP, HEAD_DIM)

    def band(a):
        ab = a.reshape(B, NB, BLOCK, N_KV_HEADS, HEAD_DIM)
        prev = jnp.pad(ab[:, :-1], ((0, 0), (1, 0), (0, 0), (0, 0), (0, 0)))
        meta = jnp.broadcast_to(a[:, pad:pad + N_META][:, None],
                                (B, NB, N_META, N_KV_HEADS, HEAD_DIM))
        return jnp.concatenate([meta, prev, ab], axis=2)

    kb, vb = band(kp), band(vp)

    meta_pos = pad + jnp.arange(N_META)
    band_pos = (jnp.arange(NB)[:, None] - 1) * BLOCK + jnp.arange(2 * BLOCK)[None, :]
    kpos = jnp.concatenate([jnp.broadcast_to(meta_pos[None], (NB, N_META)), band_pos], axis=1)
    qpos = jnp.arange(NB)[:, None] * BLOCK + jnp.arange(BLOCK)[None, :]
    dist = qpos[:, :, None] - kpos[:, None, :]
    is_meta = jnp.arange(NK) < N_META
    valid = (dist >= 0) & (is_meta[None, None, :]
                           | ((kpos[:, None, :] >= BLOCK) & (dist < WINDOW)))

    bias = rel_bias.astype(jnp.float32)[t5_causal_bucket(dist)]
    bias = bias.reshape(NB, BLOCK, NK, N_KV_HEADS, Q_GROUP).transpose(0, 3, 4, 1, 2)

    scale = HEAD_DIM ** -0.5
    s = jnp.einsum("bnqhgd,bnkhd->bnhgqk", qb, kb).astype(jnp.float32) * scale + bias[None]
    s = jnp.where(valid[None, :, None, None], s, -jnp.inf)
    sink = sinks.astype(jnp.float32).reshape(N_KV_HEADS, Q_GROUP)[None, None, :, :, None, None]
    m = jnp.maximum(jnp.max(s, axis=-1, keepdims=True), sink)
    p = jnp.exp(s - m)
    probs = p / (jnp.sum(p, axis=-1, keepdims=True) + jnp.exp(sink - m))
    o = jnp.einsum("bnhgqk,bnkhd->bnqhgd", probs.astype(vb.dtype), vb)
    return o.reshape(B, NB * BLOCK, Q_WIDTH)[:, pad:]


def hybrid_layer(h, rel_bias, norm_mix_g, w_in, pool_w, pool_scale, w_pool_br, sinks,
                 w_attn_br, w_out, norm_mlp_g, w_up, w_down):
    B, L, _ = h.shape
    u = rms_norm(h, norm_mix_g)
    z = u @ w_in
    o1 = POOL_WIDTH
    o2 = o1 + Q_WIDTH
    o3 = o2 + KV_WIDTH
    o4 = o3 + KV_WIDTH
    o5 = o4 + D_MODEL
    z_pool, z_q, z_k, z_v, z_gp, z_ga = jnp.split(z, [o1, o2, o3, o4, o5], axis=-1)

    pooled = multiscale_pool(z_pool)
    mixed = jnp.einsum("blgc,gcd->blgd", pooled, pool_w).reshape(B, L, POOL_WIDTH)
    y_pool = (mixed * pool_scale) @ w_pool_br

    q = z_q.reshape(B, L, N_Q_HEADS, HEAD_DIM)
    k = z_k.reshape(B, L, N_KV_HEADS, HEAD_DIM)
    v = z_v.reshape(B, L, N_KV_HEADS, HEAD_DIM)
    y_attn = sliding_window_attention(q, k, v, sinks, rel_bias) @ w_attn_br

    merged = jax.nn.sigmoid(z_gp) * y_pool + jax.nn.sigmoid(z_ga) * y_attn
    h = h + merged @ w_out

    u2 = rms_norm(h, norm_mlp_g)
    h = h + jnp.square(jax.nn.relu(u2 @ w_up)) @ w_down
    return h


def setup_inputs(seed: int = 0) -> dict:
    key = jax.random.key(seed)
    ks = jax.random.split(key, 16)
    f32 = jnp.float32
    nrm = lambda k, shape, s: jax.random.normal(k, shape, f32) * s
    return {
        "x": nrm(ks[0], (BATCH, SEQ, D_MODEL), 1.0),
        "meta_tokens": nrm(ks[1], (N_META, D_MODEL), 1.0),
        "rel_bias": nrm(ks[2], (N_BUCKETS, N_Q_HEADS), 0.5),
        "norm_mix_g": 1.0 + nrm(ks[3], (DEPTH, D_MODEL), 0.02),
        "w_in": nrm(ks[4], (DEPTH, D_MODEL, IN_WIDTH), D_MODEL ** -0.5),
        "pool_w": nrm(ks[5], (DEPTH, N_POOL_GROUPS, POOL_GROUP, POOL_GROUP), POOL_GROUP ** -0.5),
        "pool_scale": 1.0 + nrm(ks[6], (DEPTH, POOL_WIDTH), 0.02),
        "w_pool_br": nrm(ks[7], (DEPTH, POOL_WIDTH, D_MODEL), POOL_WIDTH ** -0.5),
        "sinks": nrm(ks[8], (DEPTH, N_Q_HEADS), 0.5),
        "w_attn_br": nrm(ks[9], (DEPTH, Q_WIDTH, D_MODEL), Q_WIDTH ** -0.5),
        "w_out": nrm(ks[10], (DEPTH, D_MODEL, D_MODEL), D_MODEL ** -0.5),
        "norm_mlp_g": 1.0 + nrm(ks[11], (DEPTH, D_MODEL), 0.02),
        "w_up": nrm(ks[12], (DEPTH, D_MODEL, D_FF), D_MODEL ** -0.5),
        "w_down": nrm(ks[13], (DEPTH, D_FF, D_MODEL), D_FF ** -0.5),
        "norm_final_g": 1.0 + nrm(ks[14], (D_MODEL,), 0.02),
    }


def reference(x, meta_tokens, rel_bias, norm_mix_g, w_in, pool_w, pool_scale, w_pool_br,
              sinks, w_attn_br, w_out, norm_mlp_g, w_up, w_down, norm_final_g):
    B = x.shape[0]
    meta = jnp.broadcast_to(meta_tokens[None].astype(x.dtype), (B, N_META, D_MODEL))
    h = jnp.concatenate([meta, x], axis=1)
    for l in range(DEPTH):
        h = hybrid_layer(h, rel_bias, norm_mix_g[l], w_in[l], pool_w[l], pool_scale[l],
                         w_pool_br[l], sinks[l], w_attn_br[l], w_out[l], norm_mlp_g[l],
                         w_up[l], w_down[l])
    h = rms_norm(h, norm_final_g)
    return h[:, N_META:]
```

```python
import math
import contextlib
import numpy as np
import concourse.bass as bass
import concourse.mybir as mybir
from concourse.bass_utils import run_bass_kernel_spmd

F32 = mybir.dt.float32
BF16 = mybir.dt.bfloat16
AF = mybir.ActivationFunctionType
ALU = mybir.AluOpType

NCORES = 8
D = 1024
TOK = 4096
NT = 8
TB = 4
TT = TB * 128
NMETA = 16
NK = 8
NZ = 5
NSLOT = 3
PF = 2
NSTG = 6
NEG = -30000.0
EPS = 1e-5

UNITS = [
    ("w_pool", 8192, 0), ("w_q", 8192, 0), ("w_kv", 4096, 0), ("w_gp", 8192, 0), ("w_ga", 8192, 0),
    ("w_pw", 2048, None), ("w_pb", 8192, 1), ("w_ab", 8192, None), ("w_out", 8192, None),
    ("w_up0", 8192, 2), ("w_up1", 8192, 2), ("w_up2", 8192, 2), ("w_up3", 8192, 2),
    ("w_dn0", 8192, None), ("w_dn1", 8192, None), ("w_dn2", 8192, None), ("w_dn3", 8192, None),
]
NU = len(UNITS)


class Prog:
    ENG = ("pe", "act", "dve", "pool", "sp")

    def __init__(self, nc):
        self.nc = nc
        self.lists = {e: [] for e in self.ENG}
        self.cnt = {}
        self.waited = {e: {} for e in self.ENG}
        self.lastw = {}
        self.readers = {}
        self.semkeys = list(self.ENG)

    def _deps(self, reads, writes):
        deps = {}

        def add(k, v):
            if deps.get(k, 0) < v:
                deps[k] = v
        for b in reads:
            if b in self.lastw:
                add(*self.lastw[b])
        for b in writes:
            if b in self.lastw:
                add(*self.lastw[b])
            for k, v in self.readers.get(b, {}).items():
                add(k, v)
        return deps

    def _record(self, key, val, reads, writes):
        for b in reads:
            self.readers.setdefault(b, {})[key] = val
        for b in writes:
            self.lastw[b] = (key, val)
            self.readers[b] = {}

    def _waits(self, eng, deps, skip_self):
        waits = []
        for k, v in deps.items():
            if skip_self and k == eng:
                continue
            if self.waited[eng].get(k, 0) >= v:
                continue
            self.waited[eng][k] = v
            waits.append((k, v))
        return waits

    def op(self, eng, fn, reads=(), writes=()):
        deps = self._deps(reads, writes)
        waits = self._waits(eng, deps, skip_self=(eng == "pe"))
        self.cnt[eng] = self.cnt.get(eng, 0) + 1
        val = self.cnt[eng]
        self.lists[eng].append((waits, fn, eng, 1))
        self._record(eng, val, reads, writes)

    def dma(self, issuer, slot, fn, reads=(), writes=()):
        key = "dma:" + slot
        if key not in self.cnt:
            self.cnt[key] = 0
            self.semkeys.append(key)
        deps = self._deps(reads, writes)
        waits = self._waits(issuer, deps, skip_self=False)
        self.cnt[key] += 16
        val = self.cnt[key]
        self.lists[issuer].append((waits, fn, key, 16))
        self._record(key, val, reads, writes)

    def final_wait(self, eng):
        waits = []
        for k in self.semkeys:
            v = self.cnt.get(k, 0)
            if v and self.waited[eng].get(k, 0) < v and k != eng:
                waits.append((k, v))
                self.waited[eng][k] = v
        self.lists[eng].append((waits, None, None, 0))

    def emit(self):
        nc = self.nc
        with contextlib.ExitStack() as st:
            sems = {}
            for k in self.semkeys:
                if self.cnt.get(k, 0) > 0:
                    sems[k] = st.enter_context(nc.semaphore(k.replace(":", "_")))
            block = st.enter_context(nc.Block())
            engobj = {"pe": "tensor", "act": "scalar", "dve": "vector", "pool": "gpsimd", "sp": "sync"}

            def body(e):
                def run(eng):
                    for waits, fn, key, inc in self.lists[e]:
                        for k, v in waits:
                            eng.wait_ge(sems[k], v)
                        if fn is not None:
                            fn(eng).then_inc(sems[key], inc)
                return run

            for e in self.ENG:
                if self.lists[e]:
                    getattr(block, engobj[e])(body(e))


def build_program(nt=NT, taps=None):
    taps = taps or set()
    nc = bass.Bass("TRN2", target_bir_lowering=False)
    P = Prog(nc)

    x_own = nc.dram_tensor("x_own", [TOK, D], F32, kind="ExternalInput").ap()
    x_pro = nc.dram_tensor("x_pro", [128 + NMETA, D], F32, kind="ExternalInput").ap()
    wsrc = [nc.dram_tensor(n, [128, sz], F32, kind="ExternalInput").ap() for n, sz, _ in UNITS]
    wbf = [nc.dram_tensor(n + "_bf", [128, sz], BF16).ap() for n, sz, _ in UNITS]
    scl_d = nc.dram_tensor("scl", [128, 24], F32, kind="ExternalInput").ap()
    gfin_d = nc.dram_tensor("gfin", [1, D], F32, kind="ExternalInput").ap()
    tabmp_d = nc.dram_tensor("tab_mp", [128, 8, 1024], F32, kind="ExternalInput").ap()
    tabcur_d = nc.dram_tensor("tab_cur", [128, 4, 512], F32, kind="ExternalInput").ap()
    cst_d = nc.dram_tensor("cst", [128, 1216], F32, kind="ExternalInput").ap()
    out_d = nc.dram_tensor("out", [TOK, D], F32, kind="ExternalOutput").ap()
    tap_outs = {}

    A = nc.alloc_sbuf_tensor
    h = A("h", [128, TB, D], F32)
    ub = A("ub", [128, 2, D], BF16)
    uT = A("uT", [128, 8, TT], BF16)
    arena = A("arena", [128, 32, TT], BF16)
    bufB = A("bufB", [128, 8, TT], BF16)
    Kr = A("Kr", [128, 2, NK, 2, 128], BF16)
    Vr = A("Vr", [128, NK, 256], BF16)
    KTm = A("KTm", [128, 2, 2, 128], BF16)
    Vm = A("Vm", [128, 256], BF16)
    zp = A("zp", [128, NZ, D], BF16)
    tmpf = A("tmpf", [128, 3, TT], F32)
    PT = A("PT", [128, 3, 3 * TT], BF16)
    Emp = A("Emp", [128, 8, 1024], BF16)
    Ecur = A("Ecur", [128, 4, 512], BF16)
    cb = A("cb", [128, 1216], BF16)
    gfin = A("gfin_sb", [128, D], F32)
    scl = A("scl_sb", [128, 24], F32)
    small = A("small", [128, 64], F32)
    epsT = A("epsT", [128, 1], F32)
    wring = A("wring", [128, NSLOT, 8192], BF16)
    stg = A("stg", [128, NSTG, 1024], F32)
    stgf = stg[:].rearrange("p a b -> p (a b)")
    ps = nc.alloc_psum_tensor("ps", [128, 8 * 512], F32)
    psb = ps[:, :].bitcast(BF16)

    uT2 = Emp[:, 0:4, :].rearrange("p a (k c) -> p (a k) c", k=2)
    UT = [uT, uT2]
    HB = [h, stg]

    def hkey(tt, b):
        return ("h", b) if tt % 2 == 0 else ("stg", b)

    def utkeys(tt, b):
        return [("uTb", tt % 2, b)] + ([("Emp", j) for j in range(4)] if tt % 2 else [])

    sqj = PT[:, 1, 0:1024]
    SQK = [("PT", 1, 0)]
    uTp = PT[:, 0, 0:8 * (128 + NMETA)].rearrange("p (k t) -> p k t", k=8)
    UTPK = [("PT", 0, 0), ("PT", 0, 1)]
    ident = cb[:, 0:128]
    Pm = cb[:, 128:1152].rearrange("p (g c t) -> p g c t", g=4, c=2)
    ones = cb[:, 1152:1216]

    def BK(i):
        return [("bank", i, 0), ("bank", i, 1)]

    def bank(i):
        return ps[:, i * 512:(i + 1) * 512]

    def bankbf(i):
        return psb[:, i * 1024:(i + 1) * 1024]

    rr = [0]

    def nextbank():
        b = rr[0]
        rr[0] = (b + 1) % 8
        return b

    def tap(name, ap, reads):
        if name not in taps:
            return
        shp = list(ap.shape)
        t = nc.dram_tensor("tap_" + name, shp, ap.dtype, kind="ExternalOutput").ap()
        tap_outs[name] = t
        P.dma("sp", "tap_" + name, lambda e, t=t, ap=ap: e.dma_start(out=t, in_=ap), reads=reads)

    P.op("dve", lambda e: e.memset(epsT[:], -0.5), writes=["epsT"])
    P.op("pool", lambda e: e.memset(Vm[:], 0.0), writes=["Vm"])
    P.op("pool", lambda e: e.memset(KTm[:], 0.0), writes=["KTm"])
    P.op("pool", lambda e: e.memset(Kr[:], 0.0), writes=["Kr0"])
    P.dma("sp", "scl", lambda e: e.dma_start(out=scl[:], in_=scl_d), writes=["scl"])
    P.dma("sp", "gfin", lambda e: e.dma_start(out=gfin[:], in_=gfin_d.partition_broadcast(128)[:, 0, :]),
          writes=["gfin"])
    P.dma("sp", "stg0", lambda e: e.dma_start(out=stgf[:, 0:1216], in_=cst_d), writes=[("stg", 0), ("stg", 1)])
    P.op("dve", lambda e: e.tensor_copy(out=cb[:], in_=stgf[:, 0:1216]), reads=[("stg", 0), ("stg", 1)], writes=["cb"])
    P.dma("sp", "stg2", lambda e: e.dma_start(out=stgf[:, 2048:4096], in_=tabcur_d.rearrange("p a b -> p (a b)")),
          writes=[("stg", 2), ("stg", 3)])
    P.op("act", lambda e: e.activation(out=Ecur[:].rearrange("p a b -> p (a b)"), in_=stgf[:, 2048:4096], func=AF.Exp),
         reads=[("stg", 2), ("stg", 3)], writes=["Ecur"])
    si = 0
    for j in range(8):
        P.dma("sp", f"stg{si}", lambda e, j=j, si=si: e.dma_start(out=stg[:, si, :], in_=tabmp_d[:, j, :]),
              writes=[("stg", si)])
        P.op("act", lambda e, j=j, si=si: e.activation(out=Emp[:, j, :], in_=stg[:, si, :], func=AF.Exp),
             reads=[("stg", si)], writes=[("Emp", j)])
        si = (si + 1) % NSTG

    W = {"issued": 0}
    cast_rr = [0]
    WQ = {"loads": [], "ops": [], "li": 0, "ci": 0, "groups_done": 0, "ngroups": 0}

    def wslot_keys(slot):
        return [("ws", slot, p) for p in range(8)]

    def emit_loads():
        while WQ["li"] < len(WQ["loads"]) and WQ["li"] < WQ["groups_done"] + NSTG:
            WQ["loads"][WQ["li"]]()
            WQ["li"] += 1

    def pump_one():
        n_, fn, closes, kind = WQ["ops"][WQ["ci"]]
        WQ["ci"] += 1
        fn()
        if closes:
            WQ["groups_done"] += 1
        emit_loads()
        return kind

    def pump():
        while WQ["ci"] < len(WQ["ops"]):
            if pump_one() == "cast":
                break

    def flush_upto(n):
        while WQ["ci"] < len(WQ["ops"]) and WQ["ops"][WQ["ci"]][0] <= n:
            pump_one()

    def issue_unit(n):
        t, u = divmod(n, NU)
        name, sz, srow = UNITS[u]
        slot = n % NSLOT
        if t == 0:
            pc = sz // 8
            per_load = max(1, 1024 // pc)
            p = 0
            while p < 8:
                npc = min(per_load, 8 - p)
                s = WQ["ngroups"] % NSTG
                WQ["ngroups"] += 1
                WQ["loads"].append(lambda s=s, u=u, p=p, npc=npc, pc=pc: P.dma(
                    "sp", f"stg{s}",
                    lambda e: e.dma_start(out=stg[:, s, 0:npc * pc], in_=wsrc[u][:, p * pc:(p + npc) * pc]),
                    writes=[("stg", s)]))
                for q in range(npc):
                    piece = p + q
                    src = stg[:, s, q * pc:(q + 1) * pc]
                    dst = wring[:, slot, piece * pc:(piece + 1) * pc]
                    eng = ("act", "dve", "act")[cast_rr[0] % 3]
                    cast_rr[0] += 1
                    if srow is None:
                        if eng == "act":
                            fn = lambda e, src=src, dst=dst: e.activation(out=dst, in_=src, func=AF.Copy)
                        else:
                            fn = lambda e, src=src, dst=dst: e.tensor_copy(out=dst, in_=src)
                    else:
                        sc = scl[:, srow * 8 + piece: srow * 8 + piece + 1]
                        if eng == "act":
                            fn = lambda e, src=src, dst=dst, sc=sc: e.activation(out=dst, in_=src, func=AF.Copy, scale=sc)
                        else:
                            fn = lambda e, src=src, dst=dst, sc=sc: e.tensor_scalar(out=dst, in0=src, scalar1=sc,
                                                                                    scalar2=None, op0=ALU.mult)
                    WQ["ops"].append((n, lambda eng=eng, fn=fn, s=s, slot=slot, piece=piece: P.op(
                        eng, fn, reads=[("stg", s), "scl"], writes=[("ws", slot, piece)]), q == npc - 1, "cast"))
                    if piece == 3 and WQ.get("held_store") is not None:
                        WQ["ops"].append((n, WQ["held_store"], False, "store"))
                        WQ["held_store"] = None
                p += npc
            if nt > 1:
                st_fn = lambda u=u, slot=slot, sz=sz: P.dma(
                    "sp", f"wst{u}", lambda e: e.dma_start(out=wbf[u], in_=wring[:, slot, 0:sz]),
                    reads=wslot_keys(slot), writes=[("wbf", u)])
                if u == NU - 1:
                    WQ["ops"].append((n, st_fn, False, "store"))
                else:
                    WQ["held_store"] = st_fn
            emit_loads()
        else:
            P.dma("sp", f"wld{slot}", lambda e, u=u, slot=slot, sz=sz: e.dma_start(out=wring[:, slot, 0:sz], in_=wbf[u]),
                  reads=[("wbf", u)], writes=wslot_keys(slot))

    def need(t, u):
        n = t * NU + u
        while W["issued"] <= min(n + PF, nt * NU - 1):
            issue_unit(W["issued"])
            W["issued"] += 1
        flush_upto(n)
        slot = n % NSLOT
        sz = UNITS[u][1]
        return wring[:, slot, 0:sz], wslot_keys(slot)

    sm_i = [0]

    def smallcols():
        i = sm_i[0]
        sm_i[0] = (i + 1) % 16
        return i

    ub_i = [0]

    def rms_part1(src, src_key, npart):
        k = smallcols()
        ss = small[0:npart, 4 * k:4 * k + 1]
        ln = small[0:npart, 4 * k + 1:4 * k + 2]
        rs = small[0:npart, 4 * k + 2:4 * k + 3]
        skey = ("small", k)
        bi = ub_i[0]
        ub_i[0] ^= 1
        u_ = ub[0:npart, bi, :]
        P.op("act", lambda e: e.activation(out=sqj[0:npart, :], in_=src, func=AF.Square, accum_out=ss),
             reads=src_key, writes=[skey] + SQK)
        P.op("pool", lambda e: e.tensor_scalar(out=ln, in0=ss, scalar1=1.0 / D, scalar2=EPS, op0=ALU.mult, op1=ALU.add),
             reads=[skey], writes=[skey])
        P.op("pool", lambda e: e.tensor_tensor(out=rs, in0=ln, in1=epsT[0:npart, :], op=ALU.pow),
             reads=[skey, "epsT"], writes=[skey])
        P.op("act", lambda e: e.activation(out=u_, in_=src, func=AF.Copy, scale=rs),
             reads=src_key + [skey], writes=[("ub", bi)])
        return (bi, u_, npart)

    def rms_part2(state, dstT, dst_col, dst_keys):
        bi, u_, npart = state
        b = nextbank()

        def tr(e):
            ins = None
            for kc in range(8):
                ins = e.transpose(bankbf(b)[:, kc * 128:kc * 128 + npart], u_[:, kc * 128:(kc + 1) * 128],
                                  ident[0:npart, 0:npart])
            return ins
        P.op("pe", tr, reads=[("ub", bi), "cb"], writes=BK(b))
        srcv = bankbf(b).rearrange("p (k t) -> p k t", k=8)[:, :, 0:npart]
        dst = dstT[:, :, dst_col:dst_col + npart]
        P.op("dve", lambda e: e.tensor_copy(out=dst, in_=srcv), reads=BK(b), writes=dst_keys)

    def rms_to_T(src, src_key, npart, dstT, dst_col, dst_keys):
        rms_part2(rms_part1(src, src_key, npart), dstT, dst_col, dst_keys)

    def s1_part1(tau, b):
        return rms_part1(HB[tau % 2][:, b, :], [hkey(tau, b)], 128)

    def s1_part2(tau, b, st):
        rms_part2(st, UT[tau % 2], b * 128, utkeys(tau, b))

    def mm_group(b, nk, lhs_fn, rhs_fn, reads, out=None):
        pump()
        o = bank(b) if out is None else out
        pairs = [(lhs_fn(kc), rhs_fn(kc)) for kc in range(nk)]

        def f(e):
            ins = None
            for kc in range(nk):
                ins = e.matmul(o, lhsT=pairs[kc][0], rhs=pairs[kc][1], start=(kc == 0), stop=(kc == nk - 1))
            return ins
        P.op("pe", f, reads=reads, writes=BK(b))

    def kslot(B):
        return (B - 1) % NK

    def zslot(B):
        return B % NZ

    out_v = out_d.rearrange("(n p) d -> n p d", p=128)
    xo_v = x_own.rearrange("(n p) d -> n p d", p=128)

    def load_x_block(tau, b):
        dst = HB[tau % 2][:, b, :]
        P.dma("sp", f"x{tau % 2}_{b}", lambda e, tau=tau, b=b, dst=dst: e.dma_start(out=dst, in_=xo_v[tau * TB + b]),
              writes=[hkey(tau, b)])

    def load_x(tau):
        for b in range(TB):
            load_x_block(tau, b)

    hp = arena[:, 24:28, :].rearrange("p a b -> p (a b)").bitcast(F32)
    hm = arena[0:NMETA, 28:32, :].rearrange("p a b -> p (a b)").bitcast(F32)
    HPK = [("ar", c) for c in range(24, 28)]
    HMK = [("ar", c) for c in range(28, 32)]
    P.dma("sp", "xm", lambda e: e.dma_start(out=hm, in_=x_pro[128:128 + NMETA, :]), writes=HMK)
    P.dma("sp", "xp", lambda e: e.dma_start(out=hp, in_=x_pro[0:128, :]), writes=HPK)
    load_x(0)

    for t in range(nt):
        first = (t == 0)
        B0 = 1 + t * TB
        hb = HB[t % 2]
        uTc = UT[t % 2]
        hoist = (t + 1 >= 2 and t + 1 < nt)

        if first:
            rms_to_T(hp, HPK, 128, uTp, 0, UTPK)
            rms_to_T(hm, HMK, NMETA, uTp, 128, UTPK)
        if t == 0:
            for b in range(TB):
                s1_part2(t, b, s1_part1(t, b))
        uT_r = [("uTb", t % 2, b) for b in range(TB)]

        w, wk = need(t, 0)
        w = w.rearrange("p (k c) -> p k c", k=8)
        blocks = ([(0, uTp, 0)] if first else []) + [(B0 + b, uTc, b * 128) for b in range(TB)]
        for (B, src, col) in blocks:
            rd = UTPK if src is uTp else [("uTb", t % 2, B - B0)]
            for half in range(2):
                bk = nextbank()
                mm_group(bk, 8, lambda kc, src=src, col=col: src[:, kc, col:col + 128],
                         lambda kc, w=w, half=half: w[:, kc, half * 512:(half + 1) * 512], reads=rd + wk)
                P.op("dve", lambda e, bk=bk, B=B, half=half: e.tensor_copy(
                    out=zp[:, zslot(B), half * 512:(half + 1) * 512], in_=bank(bk)),
                    reads=BK(bk), writes=[("zp", zslot(B), half)])
        w, wk = need(t, 1)
        w = w.rearrange("p (k c) -> p k c", k=8)
        for c in range(8):
            bk = nextbank()
            mm_group(bk, 8, lambda kc, w=w, c=c: w[:, kc, c * 128:(c + 1) * 128], lambda kc, uTc=uTc: uTc[:, kc, :],
                     reads=uT_r + wk)
            P.op("act", lambda e, bk=bk, c=c: e.activation(out=arena[:, c, :], in_=bank(bk), func=AF.Copy, scale=0.125),
                 reads=BK(bk), writes=[("ar", c)])
        w, wk = need(t, 2)
        w = w.rearrange("p (k c) -> p k c", k=8)
        if first:
            for p in range(2):
                bk = nextbank()
                mm_group(bk, 8, lambda kc, w=w, p=p: w[:, kc, p * 128:(p + 1) * 128], lambda kc: uTp[:, kc, :],
                         reads=UTPK + wk, out=bank(bk)[:, 0:128 + NMETA])
                for r in range(2):
                    pr = slice(64 * r, 64 * r + 64)
                    P.op("dve", lambda e, bk=bk, p=p, r=r, pr=pr: e.tensor_copy(out=Kr[pr, r, kslot(0), p, :],
                                                                                in_=bank(bk)[pr, 0:128]),
                         reads=BK(bk) + ["Kr0"], writes=[("K", kslot(0), p, r)])
                    P.op("dve", lambda e, bk=bk, p=p, r=r, pr=pr: e.tensor_copy(out=KTm[pr, r, p, 0:NMETA],
                                                                                in_=bank(bk)[pr, 128:128 + NMETA]),
                         reads=BK(bk) + ["KTm"], writes=[("KTm", p, r)])
            bk = nextbank()
            mm_group(bk, 8, lambda kc: uTp[:, kc, 0:128], lambda kc, w=w: w[:, kc, 256:512], reads=UTPK + wk,
                     out=bank(bk)[:, 0:256])
            P.op("dve", lambda e, bk=bk: e.tensor_copy(out=Vr[:, kslot(0), :], in_=bank(bk)[:, 0:256]),
                 reads=BK(bk), writes=[("V", kslot(0))])
            bk = nextbank()
            mm_group(bk, 8, lambda kc: uTp[:, kc, 128:128 + NMETA], lambda kc, w=w: w[:, kc, 256:512],
                     reads=UTPK + wk, out=bank(bk)[0:NMETA, 0:256])
            P.op("dve", lambda e, bk=bk: e.tensor_copy(out=Vm[0:NMETA, :], in_=bank(bk)[0:NMETA, 0:256]),
                 reads=BK(bk) + ["Vm"], writes=["Vm2"])
        for p in range(2):
            bk = nextbank()
            mm_group(bk, 8, lambda kc, w=w, p=p: w[:, kc, p * 128:(p + 1) * 128], lambda kc, uTc=uTc: uTc[:, kc, :],
                     reads=uT_r + wk)
            s0 = kslot(B0)
            for r in range(2):
                pr = slice(64 * r, 64 * r + 64)
                P.op("dve", lambda e, bk=bk, p=p, s0=s0, r=r, pr=pr: e.tensor_copy(
                    out=Kr[pr, r, s0:s0 + TB, p, :], in_=bank(bk)[pr, :].rearrange("p (b t) -> p b t", b=TB)),
                    reads=BK(bk) + ["Kr0"], writes=[("K", s0 + b, p, r) for b in range(TB)])
        for b in range(TB):
            bk = nextbank()
            mm_group(bk, 8, lambda kc, b=b, uTc=uTc: uTc[:, kc, b * 128:(b + 1) * 128], lambda kc, w=w: w[:, kc, 256:512],
                     reads=[("uTb", t % 2, b)] + wk, out=bank(bk)[:, 0:256])
            P.op("dve", lambda e, bk=bk, vs_=kslot(B0 + b): e.tensor_copy(out=Vr[:, vs_, :], in_=bank(bk)[:, 0:256]),
                 reads=BK(bk), writes=[("V", kslot(B0 + b))])
        hoist_st = {}
        for gi in range(2):
            w, wk = need(t, 3 + gi)
            w = w.rearrange("p (k c) -> p k c", k=8)
            for c in range(8):
                ci = gi * 8 + c
                if hoist and ci % 4 == 0:
                    hoist_st[ci // 4] = s1_part1(t + 1, ci // 4)
                if hoist and ci % 4 == 3:
                    s1_part2(t + 1, ci // 4, hoist_st[ci // 4])
                bk = nextbank()
                mm_group(bk, 8, lambda kc, w=w, c=c: w[:, kc, c * 128:(c + 1) * 128], lambda kc, uTc=uTc: uTc[:, kc, :],
                         reads=uT_r + wk)
                ch = 8 + gi * 8 + c
                P.op("act", lambda e, bk=bk, ch=ch: e.activation(out=arena[:, ch, :], in_=bank(bk), func=AF.Sigmoid),
                     reads=BK(bk), writes=[("ar", ch)])
        if first:
            tap("zp", zp[:, 0:5, :], [("zp", s, hf) for s in range(5) for hf in range(2)])
            tap("qt", arena[:, 0:8, :], [("ar", c) for c in range(8)])
            tap("vr", Vr[:], [("V", s) for s in (kslot(0), 0, 1, 2, 3)])
            tap("vm", Vm[:], ["Vm2"])
            tap("sg", arena[:, 8:24, :], [("ar", c) for c in range(8, 24)])

        for c in range(8):
            g = c // 2
            bk = nextbank()

            def pl(e, c=c, g=g, bk=bk, B0=B0):
                ins = None
                for b in range(TB):
                    B = B0 + b
                    o = bank(bk)[:, b * 128:(b + 1) * 128]
                    e.matmul(o, lhsT=zp[:, zslot(B), c * 128:(c + 1) * 128], rhs=Pm[:, g, 0, :], start=True, stop=False)
                    ins = e.matmul(o, lhsT=zp[:, zslot(B - 1), c * 128:(c + 1) * 128], rhs=Pm[:, g, 1, :],
                                   start=False, stop=True)
                return ins
            hf = c // 4
            pump()
            P.op("pe", pl, reads=[("zp", zslot(B0 + b), hf) for b in range(-1, TB)] + ["cb"], writes=BK(bk))
            P.op("dve", lambda e, bk=bk, c=c: e.tensor_copy(out=arena[:, 24 + c, :], in_=bank(bk)),
                 reads=BK(bk), writes=[("ar", 24 + c)])
        w, wk = need(t, 5)
        w = w.rearrange("p (g k c) -> p g k c", g=4, k=2)
        for g in range(4):
            for oc in range(2):
                bk = nextbank()
                mm_group(bk, 2, lambda kc, w=w, g=g, oc=oc: w[:, g, kc, oc * 128:(oc + 1) * 128],
                         lambda kc, g=g: arena[:, 24 + 2 * g + kc, :],
                         reads=[("ar", 24 + 2 * g), ("ar", 25 + 2 * g)] + wk)
                P.op("act", lambda e, bk=bk, g=g, oc=oc: e.activation(out=bufB[:, 2 * g + oc, :], in_=bank(bk), func=AF.Copy),
                     reads=BK(bk), writes=[("bB", 2 * g + oc)])
        if first:
            tap("pooled", arena[:, 24:32, :], [("ar", c) for c in range(24, 32)])
            tap("mixed", bufB[:], [("bB", c) for c in range(8)])
        items = [(b, kvh) for b in range(TB) for kvh in range(4)]

        AB = [(3, 4), (5, 6)]

        def emit_qk(i):
            b, kvh = items[i]
            B = B0 + b
            p, r = kvh // 2, kvh % 2
            rhs = arena[:, 4 * p:4 * p + 4, b * 128:(b + 1) * 128]

            def f1(e):
                e.matmul(bank(0), lhsT=KTm[:, r, p, :], rhs=rhs, start=True, stop=True)
                return e.matmul(bank(1), lhsT=Kr[:, r, kslot(B - 1), p, :], rhs=rhs, start=True, stop=True)

            def f2(e):
                return e.matmul(bank(2), lhsT=Kr[:, r, kslot(B), p, :], rhs=rhs, start=True, stop=True)
            qk_r = [("ar", 4 * p + g) for g in range(4)]
            P.op("pe", f1, reads=qk_r + [("KTm", p, r), ("K", kslot(B - 1), p, r), "KTm", "Kr0"], writes=BK(0) + BK(1))
            P.op("pe", f2, reads=qk_r + [("K", kslot(B), p, r), "Kr0"], writes=BK(2))
            buf = i % 3
            var = 0 if (first and b == 0) else 1
            P.op("act", lambda e: e.activation(out=PT[:, buf, 0:1024], in_=ps[:, 0:1024], func=AF.Exp),
                 reads=BK(0) + BK(1), writes=[("PT", buf, 0)])
            P.op("act", lambda e: e.activation(out=PT[:, buf, 1024:1536], in_=bank(2), func=AF.Exp),
                 reads=BK(2), writes=[("PT", buf, 1)])
            P.op("dve", lambda e: e.tensor_tensor(out=PT[:, buf, 0:1024], in0=PT[:, buf, 0:1024],
                                                  in1=Emp[:, var * 4 + kvh, :], op=ALU.mult),
                 reads=[("PT", buf, 0), ("Emp", var * 4 + kvh)], writes=[("PT", buf, 0)])
            P.op("dve", lambda e: e.tensor_tensor(out=PT[:, buf, 1024:1536], in0=PT[:, buf, 1024:1536],
                                                  in1=Ecur[:, kvh, :], op=ALU.mult),
                 reads=[("PT", buf, 1), "Ecur"], writes=[("PT", buf, 1)])

        def emit_pv(i):
            b, kvh = items[i]
            B = B0 + b
            p, r = kvh // 2, kvh % 2
            buf = i % 3
            bo, bd = AB[(i // 2) % 2]
            vs = [Vm[:, kvh * 64:(kvh + 1) * 64], Vr[:, kslot(B - 1), kvh * 64:(kvh + 1) * 64],
                  Vr[:, kslot(B), kvh * 64:(kvh + 1) * 64]]

            def f(e):
                ins = None
                for j in range(3):
                    rhs = PT[:, buf, j * 512:(j + 1) * 512]
                    e.matmul(bank(bo)[64 * r:64 * r + 64, :], lhsT=vs[j], rhs=rhs, start=(j == 0), stop=(j == 2),
                             tile_position=(0, 64 * r))
                    ins = e.matmul(bank(bd)[64 * r:64 * r + 64, :], lhsT=ones, rhs=rhs, start=(j == 0), stop=(j == 2),
                                   tile_position=(0, 64 * r))
                return ins
            P.op("pe", f, reads=[("PT", buf, 0), ("PT", buf, 1), "Vm2", ("V", kslot(B - 1)), ("V", kslot(B)), "cb"],
                 writes=[("bank", bo, r), ("bank", bd, r)])
            if r == 1:
                ti = (i // 2) % 2
                den = tmpf[:, ti, :]
                P.op("act", lambda e: e.activation(out=den, in_=bank(bd), func=AF.Ln),
                     reads=BK(bd), writes=[("tmpf", ti)])
                P.op("act", lambda e: e.activation(out=den, in_=den, func=AF.Exp, scale=-1.0),
                     reads=[("tmpf", ti)], writes=[("tmpf", ti)])
                P.op("dve", lambda e: e.tensor_tensor(
                    out=arena[:, 24 + 4 * p:24 + 4 * p + 4, b * 128:(b + 1) * 128],
                    in0=bank(bo).rearrange("p (g t) -> p g t", g=4), in1=den.rearrange("p (g t) -> p g t", g=4), op=ALU.mult),
                    reads=BK(bo) + [("tmpf", ti)], writes=[("ar", 24 + 4 * p + g) for g in range(4)])

        ypw = {}

        def emit_ypool(dcs, banks):
            if not ypw:
                w_, wk_ = need(t, 6)
                ypw["w"] = w_.rearrange("p (k c) -> p k c", k=8)
                ypw["wk"] = wk_
            w, wk = ypw["w"], ypw["wk"]
            for dc, bk in zip(dcs, banks):
                mm_group(bk, 8, lambda kc, w=w, dc=dc: w[:, kc, dc * 128:(dc + 1) * 128], lambda kc: bufB[:, kc, :],
                         reads=[("bB", c) for c in range(8)] + wk)
                P.op("dve", lambda e, bk=bk, dc=dc: e.tensor_tensor(out=arena[:, 8 + dc, :], in0=bank(bk),
                                                                     in1=arena[:, 8 + dc, :], op=ALU.mult),
                     reads=BK(bk) + [("ar", 8 + dc)], writes=[("ar", 8 + dc)])


        for i in range(len(items) + 1):
            if i < len(items):
                emit_qk(i)
            if i == 0:
                emit_ypool(range(0, 4), (3, 4, 5, 6))
            if i >= 1:
                emit_pv(i - 1)
        emit_ypool(range(4, 8), (7, 0, 1, 2))
        if first:
            tap("ot", arena[:, 24:32, :], [("ar", c) for c in range(24, 32)])

        w, wk = need(t, 7)
        w = w.rearrange("p (k c) -> p k c", k=8)
        for dc in range(8):
            bk = rr[0] = (rr[0] % 6)
            rr[0] = (bk + 1) % 8
            mm_group(bk, 8, lambda kc, w=w, dc=dc: w[:, kc, dc * 128:(dc + 1) * 128], lambda kc: arena[:, 24 + kc, :],
                     reads=[("ar", c) for c in range(24, 32)] + wk)
            tf = tmpf[:, 2, :]
            P.op("dve", lambda e, bk=bk, dc=dc: e.tensor_tensor(out=tf, in0=bank(bk), in1=arena[:, 16 + dc, :], op=ALU.mult),
                 reads=BK(bk) + [("ar", 16 + dc)], writes=[("tmpf", 2)])
            P.op("dve", lambda e, dc=dc: e.tensor_tensor(out=bufB[:, dc, :], in0=tf, in1=arena[:, 8 + dc, :], op=ALU.add),
                 reads=[("tmpf", 2), ("ar", 8 + dc)], writes=[("bB", dc)])
        if first:
            tap("merged", bufB[:], [("bB", c) for c in range(8)])

        w, wk = need(t, 8)
        w = w.rearrange("p (k c) -> p k c", k=8)
        for b in range(TB):
            for half in range(2):
                bk = nextbank()
                mm_group(bk, 8, lambda kc, b=b: bufB[:, kc, b * 128:(b + 1) * 128],
                         lambda kc, w=w, half=half: w[:, kc, half * 512:(half + 1) * 512],
                         reads=[("bB", c) for c in range(8)] + wk)
                hv = hb[:, b, half * 512:(half + 1) * 512]
                P.op("dve", lambda e, bk=bk, hv=hv: e.tensor_tensor(out=hv, in0=hv, in1=bank(bk), op=ALU.add),
                     reads=BK(bk) + [hkey(t, b)], writes=[hkey(t, b)])
        if first:
            tap("h2", h[:], [("h", b) for b in range(TB)])

        for b in range(TB):
            rms_to_T(hb[:, b, :], [hkey(t, b)], 128, uTc, b * 128, utkeys(t, b))

        for j in range(4):
            w, wk = need(t, 9 + j)
            w = w.rearrange("p (k c) -> p k c", k=8)
            for c8 in range(8):
                c = j * 8 + c8
                bk = nextbank()
                mm_group(bk, 8, lambda kc, w=w, c8=c8: w[:, kc, c8 * 128:(c8 + 1) * 128], lambda kc, uTc=uTc: uTc[:, kc, :],
                         reads=uT_r + wk)
                ti = c % 2
                P.op("act", lambda e, bk=bk, ti=ti: e.activation(out=tmpf[:, ti, :], in_=bank(bk), func=AF.Relu),
                     reads=BK(bk), writes=[("tmpf", ti)])
                P.op("pool", lambda e, c=c, ti=ti: e.tensor_tensor(out=arena[:, c, :], in0=tmpf[:, ti, :],
                                                                    in1=tmpf[:, ti, :], op=ALU.mult),
                     reads=[("tmpf", ti)], writes=[("ar", c)])

        for j in range(4):
            w, wk = need(t, 13 + j)
            w = w.rearrange("p (k c) -> p k c", k=8)
            if first and j == 2 and nt > 1:
                assert W["issued"] >= NU
                flush_upto(NU - 1)
                load_x(1)
            for b in range(TB):
                for half in range(2):
                    bk = 2 * b + half

                    def f(e, w=w, j=j, b=b, half=half, bk=bk):
                        ins = None
                        for kc in range(8):
                            ins = e.matmul(bank(bk), lhsT=arena[:, j * 8 + kc, b * 128:(b + 1) * 128],
                                           rhs=w[:, kc, half * 512:(half + 1) * 512],
                                           start=(j == 0 and kc == 0), stop=(j == 3 and kc == 7))
                        return ins
                    pump()
                    P.op("pe", f, reads=[("ar", j * 8 + kc) for kc in range(8)] + wk, writes=BK(bk))
        rr[0] = 0
        bh = (first and nt > 1)
        bst = {}
        if bh:
            bst[0] = s1_part1(1, 0)
            bst[1] = s1_part1(1, 1)
        for b in range(TB):
            hk = hkey(t, b)
            for half in range(2):
                bk = 2 * b + half
                hv = hb[:, b, half * 512:(half + 1) * 512]
                P.op("dve", lambda e, bk=bk, hv=hv: e.tensor_tensor(out=hv, in0=hv, in1=bank(bk), op=ALU.add),
                     reads=BK(bk) + [hk], writes=[hk])
            k = smallcols()
            ss = small[:, 4 * k:4 * k + 1]
            ln = small[:, 4 * k + 1:4 * k + 2]
            rs = small[:, 4 * k + 2:4 * k + 3]
            skey = ("small", k)
            hrow = hb[:, b, :]
            P.op("act", lambda e, hrow=hrow, ss=ss: e.activation(out=sqj, in_=hrow, func=AF.Square, accum_out=ss),
                 reads=[hk], writes=[skey] + SQK)
            P.op("pool", lambda e, ss=ss, ln=ln: e.tensor_scalar(out=ln, in0=ss, scalar1=1.0 / D, scalar2=EPS,
                                                                 op0=ALU.mult, op1=ALU.add),
                 reads=[skey], writes=[skey])
            P.op("pool", lambda e, ln=ln, rs=rs: e.tensor_tensor(out=rs, in0=ln, in1=epsT[:], op=ALU.pow),
                 reads=[skey, "epsT"], writes=[skey])
            P.op("dve", lambda e, hrow=hrow, rs=rs: e.scalar_tensor_tensor(out=hrow, in0=hrow, scalar=rs,
                                                                           in1=gfin[:], op0=ALU.mult, op1=ALU.mult),
                 reads=[hk, skey, "gfin"], writes=[hk])
            P.dma("sp", f"o{t % 2}_{b}", lambda e, t=t, b=b, hrow=hrow: e.dma_start(out=out_v[t * TB + b], in_=hrow),
                  reads=[hk])
            if t + 2 < nt:
                load_x_block(t + 2, b)
            if bh:
                s1_part2(1, b, bst[b])
                if b + 2 < TB:
                    bst[b + 2] = s1_part1(1, b + 2)

    P.final_wait("sp")
    P.emit()
    return nc, tap_outs


def _bucket(n):
    n = np.maximum(n, 0)
    nf = np.maximum(n, 1).astype(np.float32)
    large = 16 + (np.log(nf / np.float32(16)) / np.float32(math.log(128 / 16)) * np.float32(16)).astype(np.int32)
    large = np.minimum(large, 31)
    return np.where(n < 16, n, large).astype(np.int64)


def _kc_layout(w):
    K, C = w.shape
    return np.ascontiguousarray(w.reshape(K // 128, 128, C).transpose(1, 0, 2).reshape(128, (K // 128) * C))


def _q_perm():
    perm = []
    for p in range(2):
        for g in range(4):
            for kvh in (2 * p, 2 * p + 1):
                hd = kvh * 4 + g
                perm.extend(range(hd * 64, hd * 64 + 64))
    return np.array(perm, dtype=np.int64)


def prepare_inputs(x, meta_tokens, rel_bias, norm_mix_g, w_in, pool_w, pool_scale, w_pool_br, sinks,
                   w_attn_br, w_out, norm_mlp_g, w_up, w_down, norm_final_g):
    f = np.float32
    x = np.asarray(x, f)
    meta = np.asarray(meta_tokens, f)
    rb = np.asarray(rel_bias, f)
    w_in = np.asarray(w_in, f)[0]
    qp = _q_perm()
    shared = {}
    shared["w_pool"] = _kc_layout(w_in[:, 0:1024])
    shared["w_q"] = _kc_layout(w_in[:, 1024 + qp])
    shared["w_kv"] = _kc_layout(w_in[:, 2048:2560])
    shared["w_gp"] = _kc_layout(w_in[:, 2560:3584])
    shared["w_ga"] = _kc_layout(w_in[:, 3584:4608])
    pw = np.asarray(pool_w, f)[0]
    shared["w_pw"] = np.ascontiguousarray(pw.reshape(4, 2, 128, 256).transpose(2, 0, 1, 3).reshape(128, 2048))
    shared["w_pb"] = _kc_layout(np.asarray(w_pool_br, f)[0])
    shared["w_ab"] = _kc_layout(np.asarray(w_attn_br, f)[0][qp, :])
    shared["w_out"] = _kc_layout(np.asarray(w_out, f)[0])
    wu = np.asarray(w_up, f)[0]
    wd = np.asarray(w_down, f)[0]
    for j in range(4):
        shared[f"w_up{j}"] = _kc_layout(wu[:, j * 1024:(j + 1) * 1024])
        shared[f"w_dn{j}"] = _kc_layout(wd[j * 1024:(j + 1) * 1024, :])
    scl = np.zeros((128, 24), f)
    for i, v in enumerate((norm_mix_g, pool_scale, norm_mlp_g)):
        scl[:, i * 8:(i + 1) * 8] = np.asarray(v, f)[0].reshape(8, 128).T
    shared["scl"] = scl
    shared["gfin"] = np.asarray(norm_final_g, f).reshape(1, D)

    cst = np.zeros((128, 1216), f)
    cst[:, 0:128] = np.eye(128, dtype=f)
    pm = np.zeros((128, 4, 2, 128), f)
    tp = np.arange(128)[:, None]
    tq = np.arange(128)[None, :]
    for g, wdw in enumerate((2, 4, 8, 16)):
        d = tq - tp
        pm[:, g, 0, :] = np.where((d >= 0) & (d < wdw), 1.0 / wdw, 0.0) - (d == 0)
        d2 = tq + 128 - tp
        pm[:, g, 1, :] = np.where((d2 >= 0) & (d2 < wdw), 1.0 / wdw, 0.0)
    cst[:, 128:1152] = pm.reshape(128, 1024)
    cst[:, 1152:1216] = 1.0
    shared["cst"] = cst

    rbx = np.concatenate([rb, np.full((1, 16), NEG, f)], 0)
    sk = np.asarray(sinks, f)[0]
    kk = np.arange(128)[:, None]
    qq = np.arange(128)[None, :]
    d_cur = qq - kk
    idx_cur = np.where(d_cur >= 0, _bucket(d_cur), 32)
    d_prev = qq + 128 - kk
    idx_prev = np.where(d_prev < 128, _bucket(d_prev), 32)
    idx_masked = np.full((128, 128), 32, np.int64)
    idx_meta_std = np.full((128, 128), 32, np.int64)
    idx_meta_std[0:16, :] = 31
    idx_meta_first = np.full((128, 128), 32, np.int64)
    idx_meta_first[0:16, :] = _bucket(qq + 16 - kk[0:16])

    def expand(idx, kvh, sink_row):
        t = np.empty((128, 512), f)
        for g in range(4):
            t[:, g * 128:(g + 1) * 128] = rbx[idx, kvh * 4 + g]
            if sink_row:
                t[16, g * 128:(g + 1) * 128] = sk[kvh * 4 + g]
        return t

    tab_cur = np.stack([expand(idx_cur, kvh, False) for kvh in range(4)], 1)
    shared["tab_cur"] = np.ascontiguousarray(tab_cur)
    std = [np.concatenate([expand(idx_meta_std, kvh, True), expand(idx_prev, kvh, False)], 1) for kvh in range(4)]
    fst = [np.concatenate([expand(idx_meta_first, kvh, True), expand(idx_masked, kvh, False)], 1) for kvh in range(4)]
    tab_std = np.stack(std + std, 1)
    tab_even = np.stack(fst + std, 1)

    in_maps = []
    for c in range(NCORES):
        bi, half = c // 2, c % 2
        m = dict(shared)
        m["x_own"] = np.ascontiguousarray(x[bi, half * TOK:(half + 1) * TOK])
        xp = np.zeros((128 + NMETA, D), f)
        if half == 0:
            xp[112:128] = meta
        else:
            xp[0:128] = x[bi, TOK - 128:TOK]
        xp[128:] = meta
        m["x_pro"] = xp
        m["tab_mp"] = np.ascontiguousarray(tab_even if half == 0 else tab_std)
        in_maps.append(m)
    return in_maps


_CACHE = {}


def kernel(**inputs):
    in_maps = prepare_inputs(**inputs)
    if "nc" not in _CACHE:
        _CACHE["nc"] = build_program()[0]
    nc = _CACHE["nc"]
    res = run_bass_kernel_spmd(nc, in_maps, core_ids=list(range(NCORES)))
    out = np.empty((4, 2 * TOK, D), np.float32)
    for c in range(NCORES):
        out[c // 2, (c % 2) * TOK:(c % 2 + 1) * TOK] = res.results[c]["out"]
    return out
```

```python
import math
import contextlib
import numpy as np
import concourse.bass as bass
import concourse.mybir as mybir
from concourse.bass_utils import run_bass_kernel_spmd

F32 = mybir.dt.float32
BF16 = mybir.dt.bfloat16
AF = mybir.ActivationFunctionType
ALU = mybir.AluOpType

NCORES = 8
D = 1024
TOK = 4096
NT = 8
TB = 4
TT = TB * 128
NMETA = 16
NK = 8
NZ = 5
NSLOT = 3
PF = 2
NSTG = 6
NEG = -30000.0
EPS = 1e-5

UNITS = [
    ("w_pool", 8192, 0), ("w_q", 8192, 0), ("w_kv", 4096, 0), ("w_gp", 8192, 0), ("w_ga", 8192, 0),
    ("w_pw", 2048, None), ("w_pb", 8192, 1), ("w_ab", 8192, None), ("w_out", 8192, None),
    ("w_up0", 8192, 2), ("w_up1", 8192, 2), ("w_up2", 8192, 2), ("w_up3", 8192, 2),
    ("w_dn0", 8192, None), ("w_dn1", 8192, None), ("w_dn2", 8192, None), ("w_dn3", 8192, None),
]
NU = len(UNITS)


class Prog:
    ENG = ("pe", "act", "dve", "pool", "sp")

    def __init__(self, nc):
        self.nc = nc
        self.lists = {e: [] for e in self.ENG}
        self.cnt = {}
        self.waited = {e: {} for e in self.ENG}
        self.lastw = {}
        self.readers = {}
        self.semkeys = list(self.ENG)

    def _deps(self, reads, writes):
        deps = {}

        def add(k, v):
            if deps.get(k, 0) < v:
                deps[k] = v
        for b in reads:
            if b in self.lastw:
                add(*self.lastw[b])
        for b in writes:
            if b in self.lastw:
                add(*self.lastw[b])
            for k, v in self.readers.get(b, {}).items():
                add(k, v)
        return deps

    def _record(self, key, val, reads, writes):
        for b in reads:
            self.readers.setdefault(b, {})[key] = val
        for b in writes:
            self.lastw[b] = (key, val)
            self.readers[b] = {}

    def _waits(self, eng, deps, skip_self):
        waits = []
        for k, v in deps.items():
            if skip_self and k == eng:
                continue
            if self.waited[eng].get(k, 0) >= v:
                continue
            self.waited[eng][k] = v
            waits.append((k, v))
        return waits

    def op(self, eng, fn, reads=(), writes=()):
        deps = self._deps(reads, writes)
        waits = self._waits(eng, deps, skip_self=(eng == "pe"))
        self.cnt[eng] = self.cnt.get(eng, 0) + 1
        val = self.cnt[eng]
        self.lists[eng].append((waits, fn, eng, 1))
        self._record(eng, val, reads, writes)

    def dma(self, issuer, slot, fn, reads=(), writes=()):
        key = "dma:" + slot
        if key not in self.cnt:
            self.cnt[key] = 0
            self.semkeys.append(key)
        deps = self._deps(reads, writes)
        waits = self._waits(issuer, deps, skip_self=False)
        self.cnt[key] += 16
        val = self.cnt[key]
        self.lists[issuer].append((waits, fn, key, 16))
        self._record(key, val, reads, writes)

    def final_wait(self, eng):
        waits = []
        for k in self.semkeys:
            v = self.cnt.get(k, 0)
            if v and self.waited[eng].get(k, 0) < v and k != eng:
                waits.append((k, v))
                self.waited[eng][k] = v
        self.lists[eng].append((waits, None, None, 0))

    def emit(self):
        nc = self.nc
        with contextlib.ExitStack() as st:
            sems = {}
            for k in self.semkeys:
                if self.cnt.get(k, 0) > 0:
                    sems[k] = st.enter_context(nc.semaphore(k.replace(":", "_")))
            block = st.enter_context(nc.Block())
            engobj = {"pe": "tensor", "act": "scalar", "dve": "vector", "pool": "gpsimd", "sp": "sync"}

            def body(e):
                def run(eng):
                    for waits, fn, key, inc in self.lists[e]:
                        for k, v in waits:
                            eng.wait_ge(sems[k], v)
                        if fn is not None:
                            fn(eng).then_inc(sems[key], inc)
                return run

            for e in self.ENG:
                if self.lists[e]:
                    getattr(block, engobj[e])(body(e))


def build_program(nt=NT, taps=None):
    taps = taps or set()
    nc = bass.Bass("TRN2", target_bir_lowering=False)
    P = Prog(nc)

    x_own = nc.dram_tensor("x_own", [TOK, D], F32, kind="ExternalInput").ap()
    x_pro = nc.dram_tensor("x_pro", [128 + NMETA, D], F32, kind="ExternalInput").ap()
    wsrc = [nc.dram_tensor(n, [128, sz], F32, kind="ExternalInput").ap() for n, sz, _ in UNITS]
    wbf = [nc.dram_tensor(n + "_bf", [128, sz], BF16).ap() for n, sz, _ in UNITS]
    scl_d = nc.dram_tensor("scl", [128, 24], F32, kind="ExternalInput").ap()
    gfin_d = nc.dram_tensor("gfin", [1, D], F32, kind="ExternalInput").ap()
    tabmp_d = nc.dram_tensor("tab_mp", [128, 8, 1024], F32, kind="ExternalInput").ap()
    tabcur_d = nc.dram_tensor("tab_cur", [128, 4, 512], F32, kind="ExternalInput").ap()
    cst_d = nc.dram_tensor("cst", [128, 1216], F32, kind="ExternalInput").ap()
    out_d = nc.dram_tensor("out", [TOK, D], F32, kind="ExternalOutput").ap()
    tap_outs = {}

    A = nc.alloc_sbuf_tensor
    h = A("h", [128, TB, D], F32)
    ub = A("ub", [128, 2, D], BF16)
    uT = A("uT", [128, 8, TT], BF16)
    arena = A("arena", [128, 32, TT], BF16)
    bufB = A("bufB", [128, 8, TT], BF16)
    Kr = A("Kr", [128, 2, NK, 2, 128], BF16)
    Vr = A("Vr", [128, NK, 256], BF16)
    KTm = A("KTm", [128, 2, 2, 128], BF16)
    Vm = A("Vm", [128, 256], BF16)
    zp = A("zp", [128, NZ, D], BF16)
    tmpf = A("tmpf", [128, 3, TT], F32)
    PT = A("PT", [128, 3, 3 * TT], BF16)
    Emp = A("Emp", [128, 8, 1024], BF16)
    Ecur = A("Ecur", [128, 4, 512], BF16)
    cb = A("cb", [128, 1216], BF16)
    gfin = A("gfin_sb", [128, D], F32)
    scl = A("scl_sb", [128, 24], F32)
    small = A("small", [128, 64], F32)
    epsT = A("epsT", [128, 1], F32)
    wring = A("wring", [128, NSLOT, 8192], BF16)
    stg = A("stg", [128, NSTG, 1024], F32)
    stgf = stg[:].rearrange("p a b -> p (a b)")
    ps = nc.alloc_psum_tensor("ps", [128, 8 * 512], F32)
    psb = ps[:, :].bitcast(BF16)

    uT2 = Emp[:, 0:4, :].rearrange("p a (k c) -> p (a k) c", k=2)
    UT = [uT, uT2]
    HB = [h, stg]

    def hkey(tt, b):
        return ("h", b) if tt % 2 == 0 else ("stg", b)

    def utkeys(tt, b):
        return [("uTb", tt % 2, b)] + ([("Emp", j) for j in range(4)] if tt % 2 else [])

    sqj = PT[:, 1, 0:1024]
    SQK = [("PT", 1, 0)]
    uTp = PT[:, 0, 0:8 * (128 + NMETA)].rearrange("p (k t) -> p k t", k=8)
    UTPK = [("PT", 0, 0), ("PT", 0, 1)]
    ident = cb[:, 0:128]
    Pm = cb[:, 128:1152].rearrange("p (g c t) -> p g c t", g=4, c=2)
    ones = cb[:, 1152:1216]

    def BK(i):
        return [("bank", i, 0), ("bank", i, 1)]

    def bank(i):
        return ps[:, i * 512:(i + 1) * 512]

    def bankbf(i):
        return psb[:, i * 1024:(i + 1) * 1024]

    rr = [0]

    def nextbank():
        b = rr[0]
        rr[0] = (b + 1) % 8
        return b

    def tap(name, ap, reads):
        if name not in taps:
            return
        shp = list(ap.shape)
        t = nc.dram_tensor("tap_" + name, shp, ap.dtype, kind="ExternalOutput").ap()
        tap_outs[name] = t
        P.dma("sp", "tap_" + name, lambda e, t=t, ap=ap: e.dma_start(out=t, in_=ap), reads=reads)

    P.op("dve", lambda e: e.memset(epsT[:], -0.5), writes=["epsT"])
    P.op("pool", lambda e: e.memset(Vm[:], 0.0), writes=["Vm"])
    P.op("pool", lambda e: e.memset(KTm[:], 0.0), writes=["KTm"])
    P.op("pool", lambda e: e.memset(Kr[:], 0.0), writes=["Kr0"])
    P.dma("sp", "scl", lambda e: e.dma_start(out=scl[:], in_=scl_d), writes=["scl"])
    P.dma("sp", "gfin", lambda e: e.dma_start(out=gfin[:], in_=gfin_d.partition_broadcast(128)[:, 0, :]),
          writes=["gfin"])
    P.dma("sp", "stg0", lambda e: e.dma_start(out=stgf[:, 0:1216], in_=cst_d), writes=[("stg", 0), ("stg", 1)])
    P.op("dve", lambda e: e.tensor_copy(out=cb[:], in_=stgf[:, 0:1216]), reads=[("stg", 0), ("stg", 1)], writes=["cb"])
    P.dma("sp", "stg2", lambda e: e.dma_start(out=stgf[:, 2048:4096], in_=tabcur_d.rearrange("p a b -> p (a b)")),
          writes=[("stg", 2), ("stg", 3)])
    P.op("act", lambda e: e.activation(out=Ecur[:].rearrange("p a b -> p (a b)"), in_=stgf[:, 2048:4096], func=AF.Exp),
         reads=[("stg", 2), ("stg", 3)], writes=["Ecur"])
    si = 0
    for j in range(8):
        P.dma("sp", f"stg{si}", lambda e, j=j, si=si: e.dma_start(out=stg[:, si, :], in_=tabmp_d[:, j, :]),
              writes=[("stg", si)])
        P.op("act", lambda e, j=j, si=si: e.activation(out=Emp[:, j, :], in_=stg[:, si, :], func=AF.Exp),
             reads=[("stg", si)], writes=[("Emp", j)])
        si = (si + 1) % NSTG

    W = {"issued": 0}
    cast_rr = [0]
    WQ = {"loads": [], "ops": [], "li": 0, "ci": 0, "groups_done": 0, "ngroups": 0}

    def wslot_keys(slot):
        return [("ws", slot, p) for p in range(8)]

    def emit_loads():
        while WQ["li"] < len(WQ["loads"]) and WQ["li"] < WQ["groups_done"] + NSTG:
            WQ["loads"][WQ["li"]]()
            WQ["li"] += 1

    def pump_one():
        n_, fn, closes, kind = WQ["ops"][WQ["ci"]]
        WQ["ci"] += 1
        fn()
        if closes:
            WQ["groups_done"] += 1
        emit_loads()
        return kind

    def pump():
        while WQ["ci"] < len(WQ["ops"]):
            if pump_one() == "cast":
                break

    def flush_upto(n):
        while WQ["ci"] < len(WQ["ops"]) and WQ["ops"][WQ["ci"]][0] <= n:
            pump_one()

    def issue_unit(n):
        t, u = divmod(n, NU)
        name, sz, srow = UNITS[u]
        slot = n % NSLOT
        if t == 0:
            pc = sz // 8
            per_load = max(1, 1024 // pc)
            p = 0
            while p < 8:
                npc = min(per_load, 8 - p)
                s = WQ["ngroups"] % NSTG
                WQ["ngroups"] += 1
                WQ["loads"].append(lambda s=s, u=u, p=p, npc=npc, pc=pc: P.dma(
                    "sp", f"stg{s}",
                    lambda e: e.dma_start(out=stg[:, s, 0:npc * pc], in_=wsrc[u][:, p * pc:(p + npc) * pc]),
                    writes=[("stg", s)]))
                for q in range(npc):
                    piece = p + q
                    src = stg[:, s, q * pc:(q + 1) * pc]
                    dst = wring[:, slot, piece * pc:(piece + 1) * pc]
                    eng = ("act", "dve", "act")[cast_rr[0] % 3]
                    cast_rr[0] += 1
                    if srow is None:
                        if eng == "act":
                            fn = lambda e, src=src, dst=dst: e.activation(out=dst, in_=src, func=AF.Copy)
                        else:
                            fn = lambda e, src=src, dst=dst: e.tensor_copy(out=dst, in_=src)
                    else:
                        sc = scl[:, srow * 8 + piece: srow * 8 + piece + 1]
                        if eng == "act":
                            fn = lambda e, src=src, dst=dst, sc=sc: e.activation(out=dst, in_=src, func=AF.Copy, scale=sc)
                        else:
                            fn = lambda e, src=src, dst=dst, sc=sc: e.tensor_scalar(out=dst, in0=src, scalar1=sc,
                                                                                    scalar2=None, op0=ALU.mult)
                    WQ["ops"].append((n, lambda eng=eng, fn=fn, s=s, slot=slot, piece=piece: P.op(
                        eng, fn, reads=[("stg", s), "scl"], writes=[("ws", slot, piece)]), q == npc - 1, "cast"))
                    if piece == 3 and WQ.get("held_store") is not None:
                        WQ["ops"].append((n, WQ["held_store"], False, "store"))
                        WQ["held_store"] = None
                p += npc
            if nt > 1:
                st_fn = lambda u=u, slot=slot, sz=sz: P.dma(
                    "sp", f"wst{u}", lambda e: e.dma_start(out=wbf[u], in_=wring[:, slot, 0:sz]),
                    reads=wslot_keys(slot), writes=[("wbf", u)])
                if u == NU - 1:
                    WQ["ops"].append((n, st_fn, False, "store"))
                else:
                    WQ["held_store"] = st_fn
            emit_loads()
        else:
            P.dma("sp", f"wld{slot}", lambda e, u=u, slot=slot, sz=sz: e.dma_start(out=wring[:, slot, 0:sz], in_=wbf[u]),
                  reads=[("wbf", u)], writes=wslot_keys(slot))

    def need(t, u):
        n = t * NU + u
        while W["issued"] <= min(n + PF, nt * NU - 1):
            issue_unit(W["issued"])
            W["issued"] += 1
        flush_upto(n)
        slot = n % NSLOT
        sz = UNITS[u][1]
        return wring[:, slot, 0:sz], wslot_keys(slot)

    sm_i = [0]

    def smallcols():
        i = sm_i[0]
        sm_i[0] = (i + 1) % 16
        return i

    ub_i = [0]

    def rms_part1(src, src_key, npart):
        k = smallcols()
        ss = small[0:npart, 4 * k:4 * k + 1]
        ln = small[0:npart, 4 * k + 1:4 * k + 2]
        rs = small[0:npart, 4 * k + 2:4 * k + 3]
        skey = ("small", k)
        bi = ub_i[0]
        ub_i[0] ^= 1
        u_ = ub[0:npart, bi, :]
        P.op("act", lambda e: e.activation(out=sqj[0:npart, :], in_=src, func=AF.Square, accum_out=ss),
             reads=src_key, writes=[skey] + SQK)
        P.op("pool", lambda e: e.tensor_scalar(out=ln, in0=ss, scalar1=1.0 / D, scalar2=EPS, op0=ALU.mult, op1=ALU.add),
             reads=[skey], writes=[skey])
        P.op("pool", lambda e: e.tensor_tensor(out=rs, in0=ln, in1=epsT[0:npart, :], op=ALU.pow),
             reads=[skey, "epsT"], writes=[skey])
        P.op("act", lambda e: e.activation(out=u_, in_=src, func=AF.Copy, scale=rs),
             reads=src_key + [skey], writes=[("ub", bi)])
        return (bi, u_, npart)

    def rms_part2(state, dstT, dst_col, dst_keys):
        bi, u_, npart = state
        b = nextbank()

        def tr(e):
            ins = None
            for kc in range(8):
                ins = e.transpose(bankbf(b)[:, kc * 128:kc * 128 + npart], u_[:, kc * 128:(kc + 1) * 128],
                                  ident[0:npart, 0:npart])
            return ins
        P.op("pe", tr, reads=[("ub", bi), "cb"], writes=BK(b))
        srcv = bankbf(b).rearrange("p (k t) -> p k t", k=8)[:, :, 0:npart]
        dst = dstT[:, :, dst_col:dst_col + npart]
        P.op("dve", lambda e: e.tensor_copy(out=dst, in_=srcv), reads=BK(b), writes=dst_keys)

    def rms_to_T(src, src_key, npart, dstT, dst_col, dst_keys):
        rms_part2(rms_part1(src, src_key, npart), dstT, dst_col, dst_keys)

    def s1_part1(tau, b):
        return rms_part1(HB[tau % 2][:, b, :], [hkey(tau, b)], 128)

    def s1_part2(tau, b, st):
        rms_part2(st, UT[tau % 2], b * 128, utkeys(tau, b))

    def mm_group(b, nk, lhs_fn, rhs_fn, reads, out=None):
        pump()
        o = bank(b) if out is None else out
        pairs = [(lhs_fn(kc), rhs_fn(kc)) for kc in range(nk)]

        def f(e):
            ins = None
            for kc in range(nk):
                ins = e.matmul(o, lhsT=pairs[kc][0], rhs=pairs[kc][1], start=(kc == 0), stop=(kc == nk - 1))
            return ins
        P.op("pe", f, reads=reads, writes=BK(b))

    def kslot(B):
        return (B - 1) % NK

    def zslot(B):
        return B % NZ

    out_v = out_d.rearrange("(n p) d -> n p d", p=128)
    xo_v = x_own.rearrange("(n p) d -> n p d", p=128)

    def load_x_block(tau, b):
        dst = HB[tau % 2][:, b, :]
        P.dma("sp", f"x{tau % 2}_{b}", lambda e, tau=tau, b=b, dst=dst: e.dma_start(out=dst, in_=xo_v[tau * TB + b]),
              writes=[hkey(tau, b)])

    def load_x(tau):
        for b in range(TB):
            load_x_block(tau, b)

    hp = arena[:, 24:28, :].rearrange("p a b -> p (a b)").bitcast(F32)
    hm = arena[0:NMETA, 28:32, :].rearrange("p a b -> p (a b)").bitcast(F32)
    HPK = [("ar", c) for c in range(24, 28)]
    HMK = [("ar", c) for c in range(28, 32)]
    P.dma("sp", "xm", lambda e: e.dma_start(out=hm, in_=x_pro[128:128 + NMETA, :]), writes=HMK)
    P.dma("sp", "xp", lambda e: e.dma_start(out=hp, in_=x_pro[0:128, :]), writes=HPK)
    load_x(0)

    for t in range(nt):
        first = (t == 0)
        B0 = 1 + t * TB
        hb = HB[t % 2]
        uTc = UT[t % 2]
        hoist = (t + 1 >= 2 and t + 1 < nt)

        if first:
            rms_to_T(hp, HPK, 128, uTp, 0, UTPK)
            rms_to_T(hm, HMK, NMETA, uTp, 128, UTPK)
        if t == 0:
            for b in range(TB):
                s1_part2(t, b, s1_part1(t, b))
        uT_r = [("uTb", t % 2, b) for b in range(TB)]

        w, wk = need(t, 0)
        w = w.rearrange("p (k c) -> p k c", k=8)
        blocks = ([(0, uTp, 0)] if first else []) + [(B0 + b, uTc, b * 128) for b in range(TB)]
        for (B, src, col) in blocks:
            rd = UTPK if src is uTp else [("uTb", t % 2, B - B0)]
            for half in range(2):
                bk = nextbank()
                mm_group(bk, 8, lambda kc, src=src, col=col: src[:, kc, col:col + 128],
                         lambda kc, w=w, half=half: w[:, kc, half * 512:(half + 1) * 512], reads=rd + wk)
                P.op("dve", lambda e, bk=bk, B=B, half=half: e.tensor_copy(
                    out=zp[:, zslot(B), half * 512:(half + 1) * 512], in_=bank(bk)),
                    reads=BK(bk), writes=[("zp", zslot(B), half)])
        w, wk = need(t, 1)
        w = w.rearrange("p (k c) -> p k c", k=8)
        for c in range(8):
            bk = nextbank()
            mm_group(bk, 8, lambda kc, w=w, c=c: w[:, kc, c * 128:(c + 1) * 128], lambda kc, uTc=uTc: uTc[:, kc, :],
                     reads=uT_r + wk)
            P.op("act", lambda e, bk=bk, c=c: e.activation(out=arena[:, c, :], in_=bank(bk), func=AF.Copy, scale=0.125),
                 reads=BK(bk), writes=[("ar", c)])
        w, wk = need(t, 2)
        w = w.rearrange("p (k c) -> p k c", k=8)
        if first:
            for p in range(2):
                bk = nextbank()
                mm_group(bk, 8, lambda kc, w=w, p=p: w[:, kc, p * 128:(p + 1) * 128], lambda kc: uTp[:, kc, :],
                         reads=UTPK + wk, out=bank(bk)[:, 0:128 + NMETA])
                for r in range(2):
                    pr = slice(64 * r, 64 * r + 64)
                    P.op("dve", lambda e, bk=bk, p=p, r=r, pr=pr: e.tensor_copy(out=Kr[pr, r, kslot(0), p, :],
                                                                                in_=bank(bk)[pr, 0:128]),
                         reads=BK(bk) + ["Kr0"], writes=[("K", kslot(0), p, r)])
                    P.op("dve", lambda e, bk=bk, p=p, r=r, pr=pr: e.tensor_copy(out=KTm[pr, r, p, 0:NMETA],
                                                                                in_=bank(bk)[pr, 128:128 + NMETA]),
                         reads=BK(bk) + ["KTm"], writes=[("KTm", p, r)])
            bk = nextbank()
            mm_group(bk, 8, lambda kc: uTp[:, kc, 0:128], lambda kc, w=w: w[:, kc, 256:512], reads=UTPK + wk,
                     out=bank(bk)[:, 0:256])
            P.op("dve", lambda e, bk=bk: e.tensor_copy(out=Vr[:, kslot(0), :], in_=bank(bk)[:, 0:256]),
                 reads=BK(bk), writes=[("V", kslot(0))])
            bk = nextbank()
            mm_group(bk, 8, lambda kc: uTp[:, kc, 128:128 + NMETA], lambda kc, w=w: w[:, kc, 256:512],
                     reads=UTPK + wk, out=bank(bk)[0:NMETA, 0:256])
            P.op("dve", lambda e, bk=bk: e.tensor_copy(out=Vm[0:NMETA, :], in_=bank(bk)[0:NMETA, 0:256]),
                 reads=BK(bk) + ["Vm"], writes=["Vm2"])
        for p in range(2):
            bk = nextbank()
            mm_group(bk, 8, lambda kc, w=w, p=p: w[:, kc, p * 128:(p + 1) * 128], lambda kc, uTc=uTc: uTc[:, kc, :],
                     reads=uT_r + wk)
            s0 = kslot(B0)
            for r in range(2):
                pr = slice(64 * r, 64 * r + 64)
                P.op("dve", lambda e, bk=bk, p=p, s0=s0, r=r, pr=pr: e.tensor_copy(
                    out=Kr[pr, r, s0:s0 + TB, p, :], in_=bank(bk)[pr, :].rearrange("p (b t) -> p b t", b=TB)),
                    reads=BK(bk) + ["Kr0"], writes=[("K", s0 + b, p, r) for b in range(TB)])
        for b in range(TB):
            bk = nextbank()
            mm_group(bk, 8, lambda kc, b=b, uTc=uTc: uTc[:, kc, b * 128:(b + 1) * 128], lambda kc, w=w: w[:, kc, 256:512],
                     reads=[("uTb", t % 2, b)] + wk, out=bank(bk)[:, 0:256])
            P.op("dve", lambda e, bk=bk, vs_=kslot(B0 + b): e.tensor_copy(out=Vr[:, vs_, :], in_=bank(bk)[:, 0:256]),
                 reads=BK(bk), writes=[("V", kslot(B0 + b))])
        hoist_st = {}
        for gi in range(2):
            w, wk = need(t, 3 + gi)
            w = w.rearrange("p (k c) -> p k c", k=8)
            for c in range(8):
                ci = gi * 8 + c
                if hoist and ci % 4 == 0:
                    hoist_st[ci // 4] = s1_part1(t + 1, ci // 4)
                if hoist and ci % 4 == 3:
                    s1_part2(t + 1, ci // 4, hoist_st[ci // 4])
                bk = nextbank()
                mm_group(bk, 8, lambda kc, w=w, c=c: w[:, kc, c * 128:(c + 1) * 128], lambda kc, uTc=uTc: uTc[:, kc, :],
                         reads=uT_r + wk)
                ch = 8 + gi * 8 + c
                P.op("act", lambda e, bk=bk, ch=ch: e.activation(out=arena[:, ch, :], in_=bank(bk), func=AF.Sigmoid),
                     reads=BK(bk), writes=[("ar", ch)])
        if first:
            tap("zp", zp[:, 0:5, :], [("zp", s, hf) for s in range(5) for hf in range(2)])
            tap("qt", arena[:, 0:8, :], [("ar", c) for c in range(8)])
            tap("vr", Vr[:], [("V", s) for s in (kslot(0), 0, 1, 2, 3)])
            tap("vm", Vm[:], ["Vm2"])
            tap("sg", arena[:, 8:24, :], [("ar", c) for c in range(8, 24)])

        for c in range(8):
            g = c // 2
            bk = nextbank()

            def pl(e, c=c, g=g, bk=bk, B0=B0):
                ins = None
                for b in range(TB):
                    B = B0 + b
                    o = bank(bk)[:, b * 128:(b + 1) * 128]
                    e.matmul(o, lhsT=zp[:, zslot(B), c * 128:(c + 1) * 128], rhs=Pm[:, g, 0, :], start=True, stop=False)
                    ins = e.matmul(o, lhsT=zp[:, zslot(B - 1), c * 128:(c + 1) * 128], rhs=Pm[:, g, 1, :],
                                   start=False, stop=True)
                return ins
            hf = c // 4
            pump()
            P.op("pe", pl, reads=[("zp", zslot(B0 + b), hf) for b in range(-1, TB)] + ["cb"], writes=BK(bk))
            P.op("dve", lambda e, bk=bk, c=c: e.tensor_copy(out=arena[:, 24 + c, :], in_=bank(bk)),
                 reads=BK(bk), writes=[("ar", 24 + c)])
        w, wk = need(t, 5)
        w = w.rearrange("p (g k c) -> p g k c", g=4, k=2)
        for g in range(4):
            for oc in range(2):
                bk = nextbank()
                mm_group(bk, 2, lambda kc, w=w, g=g, oc=oc: w[:, g, kc, oc * 128:(oc + 1) * 128],
                         lambda kc, g=g: arena[:, 24 + 2 * g + kc, :],
                         reads=[("ar", 24 + 2 * g), ("ar", 25 + 2 * g)] + wk)
                P.op("act", lambda e, bk=bk, g=g, oc=oc: e.activation(out=bufB[:, 2 * g + oc, :], in_=bank(bk), func=AF.Copy),
                     reads=BK(bk), writes=[("bB", 2 * g + oc)])
        if first:
            tap("pooled", arena[:, 24:32, :], [("ar", c) for c in range(24, 32)])
            tap("mixed", bufB[:], [("bB", c) for c in range(8)])
        items = [(b, kvh) for b in range(TB) for kvh in range(4)]

        AB = [(3, 4), (5, 6)]

        def emit_qk(i):
            b, kvh = items[i]
            B = B0 + b
            p, r = kvh // 2, kvh % 2
            rhs = arena[:, 4 * p:4 * p + 4, b * 128:(b + 1) * 128]

            def f1(e):
                e.matmul(bank(0), lhsT=KTm[:, r, p, :], rhs=rhs, start=True, stop=True)
                return e.matmul(bank(1), lhsT=Kr[:, r, kslot(B - 1), p, :], rhs=rhs, start=True, stop=True)

            def f2(e):
                return e.matmul(bank(2), lhsT=Kr[:, r, kslot(B), p, :], rhs=rhs, start=True, stop=True)
            qk_r = [("ar", 4 * p + g) for g in range(4)]
            P.op("pe", f1, reads=qk_r + [("KTm", p, r), ("K", kslot(B - 1), p, r), "KTm", "Kr0"], writes=BK(0) + BK(1))
            P.op("pe", f2, reads=qk_r + [("K", kslot(B), p, r), "Kr0"], writes=BK(2))
            buf = i % 3
            var = 0 if (first and b == 0) else 1
            P.op("act", lambda e: e.activation(out=PT[:, buf, 0:1024], in_=ps[:, 0:1024], func=AF.Exp),
                 reads=BK(0) + BK(1), writes=[("PT", buf, 0)])
            P.op("act", lambda e: e.activation(out=PT[:, buf, 1024:1536], in_=bank(2), func=AF.Exp),
                 reads=BK(2), writes=[("PT", buf, 1)])
            P.op("dve", lambda e: e.tensor_tensor(out=PT[:, buf, 0:1024], in0=PT[:, buf, 0:1024],
                                                  in1=Emp[:, var * 4 + kvh, :], op=ALU.mult),
                 reads=[("PT", buf, 0), ("Emp", var * 4 + kvh)], writes=[("PT", buf, 0)])
            P.op("dve", lambda e: e.tensor_tensor(out=PT[:, buf, 1024:1536], in0=PT[:, buf, 1024:1536],
                                                  in1=Ecur[:, kvh, :], op=ALU.mult),
                 reads=[("PT", buf, 1), "Ecur"], writes=[("PT", buf, 1)])

        def emit_pv(i):
            b, kvh = items[i]
            B = B0 + b
            p, r = kvh // 2, kvh % 2
            buf = i % 3
            bo, bd = AB[(i // 2) % 2]
            vs = [Vm[:, kvh * 64:(kvh + 1) * 64], Vr[:, kslot(B - 1), kvh * 64:(kvh + 1) * 64],
                  Vr[:, kslot(B), kvh * 64:(kvh + 1) * 64]]

            def f(e):
                ins = None
                for j in range(3):
                    rhs = PT[:, buf, j * 512:(j + 1) * 512]
                    e.matmul(bank(bo)[64 * r:64 * r + 64, :], lhsT=vs[j], rhs=rhs, start=(j == 0), stop=(j == 2),
                             tile_position=(0, 64 * r))
                    ins = e.matmul(bank(bd)[64 * r:64 * r + 64, :], lhsT=ones, rhs=rhs, start=(j == 0), stop=(j == 2),
                                   tile_position=(0, 64 * r))
                return ins
            P.op("pe", f, reads=[("PT", buf, 0), ("PT", buf, 1), "Vm2", ("V", kslot(B - 1)), ("V", kslot(B)), "cb"],
                 writes=[("bank", bo, r), ("bank", bd, r)])
            if r == 1:
                deferred.append(lambda i=i, b=b, p=p, bo=bo, bd=bd: emit_norm(i, b, p, bo, bd))

        deferred = []

        def emit_norm(i, b, p, bo, bd):
            if True:
                ti = (i // 2) % 2
                den = tmpf[:, ti, :]
                P.op("act", lambda e: e.activation(out=den, in_=bank(bd), func=AF.Ln),
                     reads=BK(bd), writes=[("tmpf", ti)])
                P.op("act", lambda e: e.activation(out=den, in_=den, func=AF.Exp, scale=-1.0),
                     reads=[("tmpf", ti)], writes=[("tmpf", ti)])
                P.op("dve", lambda e: e.tensor_tensor(
                    out=arena[:, 24 + 4 * p:24 + 4 * p + 4, b * 128:(b + 1) * 128],
                    in0=bank(bo).rearrange("p (g t) -> p g t", g=4), in1=den.rearrange("p (g t) -> p g t", g=4), op=ALU.mult),
                    reads=BK(bo) + [("tmpf", ti)], writes=[("ar", 24 + 4 * p + g) for g in range(4)])

        ypw = {}

        def emit_ypool(dcs, banks):
            if not ypw:
                w_, wk_ = need(t, 6)
                ypw["w"] = w_.rearrange("p (k c) -> p k c", k=8)
                ypw["wk"] = wk_
            w, wk = ypw["w"], ypw["wk"]
            for dc, bk in zip(dcs, banks):
                mm_group(bk, 8, lambda kc, w=w, dc=dc: w[:, kc, dc * 128:(dc + 1) * 128], lambda kc: bufB[:, kc, :],
                         reads=[("bB", c) for c in range(8)] + wk)
                P.op("dve", lambda e, bk=bk, dc=dc: e.tensor_tensor(out=arena[:, 8 + dc, :], in0=bank(bk),
                                                                     in1=arena[:, 8 + dc, :], op=ALU.mult),
                     reads=BK(bk) + [("ar", 8 + dc)], writes=[("ar", 8 + dc)])


        for i in range(len(items) + 1):
            if i < len(items):
                emit_qk(i)
            if i == 0:
                emit_ypool(range(0, 4), (3, 4, 5, 6))
            while deferred:
                deferred.pop(0)()
            if i >= 1:
                emit_pv(i - 1)
        while deferred:
            deferred.pop(0)()
        emit_ypool(range(4, 8), (7, 0, 1, 2))
        if first:
            tap("ot", arena[:, 24:32, :], [("ar", c) for c in range(24, 32)])

        w, wk = need(t, 7)
        w = w.rearrange("p (k c) -> p k c", k=8)
        for dc in range(8):
            bk = rr[0] = (rr[0] % 6)
            rr[0] = (bk + 1) % 8
            mm_group(bk, 8, lambda kc, w=w, dc=dc: w[:, kc, dc * 128:(dc + 1) * 128], lambda kc: arena[:, 24 + kc, :],
                     reads=[("ar", c) for c in range(24, 32)] + wk)
            tf = tmpf[:, 2, :]
            P.op("dve", lambda e, bk=bk, dc=dc: e.tensor_tensor(out=tf, in0=bank(bk), in1=arena[:, 16 + dc, :], op=ALU.mult),
                 reads=BK(bk) + [("ar", 16 + dc)], writes=[("tmpf", 2)])
            P.op("dve", lambda e, dc=dc: e.tensor_tensor(out=bufB[:, dc, :], in0=tf, in1=arena[:, 8 + dc, :], op=ALU.add),
                 reads=[("tmpf", 2), ("ar", 8 + dc)], writes=[("bB", dc)])
        if first:
            tap("merged", bufB[:], [("bB", c) for c in range(8)])

        w, wk = need(t, 8)
        w = w.rearrange("p (k c) -> p k c", k=8)
        for b in range(TB):
            for half in range(2):
                bk = nextbank()
                mm_group(bk, 8, lambda kc, b=b: bufB[:, kc, b * 128:(b + 1) * 128],
                         lambda kc, w=w, half=half: w[:, kc, half * 512:(half + 1) * 512],
                         reads=[("bB", c) for c in range(8)] + wk)
                hv = hb[:, b, half * 512:(half + 1) * 512]
                P.op("dve", lambda e, bk=bk, hv=hv: e.tensor_tensor(out=hv, in0=hv, in1=bank(bk), op=ALU.add),
                     reads=BK(bk) + [hkey(t, b)], writes=[hkey(t, b)])
        if first:
            tap("h2", h[:], [("h", b) for b in range(TB)])

        for b in range(TB):
            rms_to_T(hb[:, b, :], [hkey(t, b)], 128, uTc, b * 128, utkeys(t, b))

        for j in range(4):
            w, wk = need(t, 9 + j)
            w = w.rearrange("p (k c) -> p k c", k=8)
            for c8 in range(8):
                c = j * 8 + c8
                bk = nextbank()
                mm_group(bk, 8, lambda kc, w=w, c8=c8: w[:, kc, c8 * 128:(c8 + 1) * 128], lambda kc, uTc=uTc: uTc[:, kc, :],
                         reads=uT_r + wk)
                ti = c % 2
                P.op("act", lambda e, bk=bk, ti=ti: e.activation(out=tmpf[:, ti, :], in_=bank(bk), func=AF.Relu),
                     reads=BK(bk), writes=[("tmpf", ti)])
                P.op("pool", lambda e, c=c, ti=ti: e.tensor_tensor(out=arena[:, c, :], in0=tmpf[:, ti, :],
                                                                    in1=tmpf[:, ti, :], op=ALU.mult),
                     reads=[("tmpf", ti)], writes=[("ar", c)])

        for j in range(4):
            w, wk = need(t, 13 + j)
            w = w.rearrange("p (k c) -> p k c", k=8)
            if first and j == 2 and nt > 1:
                assert W["issued"] >= NU
                flush_upto(NU - 1)
                load_x(1)
            for b in range(TB):
                for half in range(2):
                    bk = 2 * b + half

                    def f(e, w=w, j=j, b=b, half=half, bk=bk):
                        ins = None
                        for kc in range(8):
                            ins = e.matmul(bank(bk), lhsT=arena[:, j * 8 + kc, b * 128:(b + 1) * 128],
                                           rhs=w[:, kc, half * 512:(half + 1) * 512],
                                           start=(j == 0 and kc == 0), stop=(j == 3 and kc == 7))
                        return ins
                    pump()
                    P.op("pe", f, reads=[("ar", j * 8 + kc) for kc in range(8)] + wk, writes=BK(bk))
        rr[0] = 0
        bh = (first and nt > 1)
        bst = {}
        if bh:
            bst[0] = s1_part1(1, 0)
            bst[1] = s1_part1(1, 1)
        for b in range(TB):
            hk = hkey(t, b)
            for half in range(2):
                bk = 2 * b + half
                hv = hb[:, b, half * 512:(half + 1) * 512]
                P.op("dve", lambda e, bk=bk, hv=hv: e.tensor_tensor(out=hv, in0=hv, in1=bank(bk), op=ALU.add),
                     reads=BK(bk) + [hk], writes=[hk])
            k = smallcols()
            ss = small[:, 4 * k:4 * k + 1]
            ln = small[:, 4 * k + 1:4 * k + 2]
            rs = small[:, 4 * k + 2:4 * k + 3]
            skey = ("small", k)
            hrow = hb[:, b, :]
            P.op("act", lambda e, hrow=hrow, ss=ss: e.activation(out=sqj, in_=hrow, func=AF.Square, accum_out=ss),
                 reads=[hk], writes=[skey] + SQK)
            P.op("pool", lambda e, ss=ss, ln=ln: e.tensor_scalar(out=ln, in0=ss, scalar1=1.0 / D, scalar2=EPS,
                                                                 op0=ALU.mult, op1=ALU.add),
                 reads=[skey], writes=[skey])
            P.op("pool", lambda e, ln=ln, rs=rs: e.tensor_tensor(out=rs, in0=ln, in1=epsT[:], op=ALU.pow),
                 reads=[skey, "epsT"], writes=[skey])
            P.op("dve", lambda e, hrow=hrow, rs=rs: e.scalar_tensor_tensor(out=hrow, in0=hrow, scalar=rs,
                                                                           in1=gfin[:], op0=ALU.mult, op1=ALU.mult),
                 reads=[hk, skey, "gfin"], writes=[hk])
            P.dma("sp", f"o{t % 2}_{b}", lambda e, t=t, b=b, hrow=hrow: e.dma_start(out=out_v[t * TB + b], in_=hrow),
                  reads=[hk])
            if t + 2 < nt:
                load_x_block(t + 2, b)
            if bh:
                s1_part2(1, b, bst[b])
                if b + 2 < TB:
                    bst[b + 2] = s1_part1(1, b + 2)

    P.final_wait("sp")
    P.emit()
    return nc, tap_outs


def _bucket(n):
    n = np.maximum(n, 0)
    nf = np.maximum(n, 1).astype(np.float32)
    large = 16 + (np.log(nf / np.float32(16)) / np.float32(math.log(128 / 16)) * np.float32(16)).astype(np.int32)
    large = np.minimum(large, 31)
    return np.where(n < 16, n, large).astype(np.int64)


def _kc_layout(w):
    K, C = w.shape
    return np.ascontiguousarray(w.reshape(K // 128, 128, C).transpose(1, 0, 2).reshape(128, (K // 128) * C))


def _q_perm():
    perm = []
    for p in range(2):
        for g in range(4):
            for kvh in (2 * p, 2 * p + 1):
                hd = kvh * 4 + g
                perm.extend(range(hd * 64, hd * 64 + 64))
    return np.array(perm, dtype=np.int64)


def prepare_inputs(x, meta_tokens, rel_bias, norm_mix_g, w_in, pool_w, pool_scale, w_pool_br, sinks,
                   w_attn_br, w_out, norm_mlp_g, w_up, w_down, norm_final_g):
    f = np.float32
    x = np.asarray(x, f)
    meta = np.asarray(meta_tokens, f)
    rb = np.asarray(rel_bias, f)
    w_in = np.asarray(w_in, f)[0]
    qp = _q_perm()
    shared = {}
    shared["w_pool"] = _kc_layout(w_in[:, 0:1024])
    shared["w_q"] = _kc_layout(w_in[:, 1024 + qp])
    shared["w_kv"] = _kc_layout(w_in[:, 2048:2560])
    shared["w_gp"] = _kc_layout(w_in[:, 2560:3584])
    shared["w_ga"] = _kc_layout(w_in[:, 3584:4608])
    pw = np.asarray(pool_w, f)[0]
    shared["w_pw"] = np.ascontiguousarray(pw.reshape(4, 2, 128, 256).transpose(2, 0, 1, 3).reshape(128, 2048))
    shared["w_pb"] = _kc_layout(np.asarray(w_pool_br, f)[0])
    shared["w_ab"] = _kc_layout(np.asarray(w_attn_br, f)[0][qp, :])
    shared["w_out"] = _kc_layout(np.asarray(w_out, f)[0])
    wu = np.asarray(w_up, f)[0]
    wd = np.asarray(w_down, f)[0]
    for j in range(4):
        shared[f"w_up{j}"] = _kc_layout(wu[:, j * 1024:(j + 1) * 1024])
        shared[f"w_dn{j}"] = _kc_layout(wd[j * 1024:(j + 1) * 1024, :])
    scl = np.zeros((128, 24), f)
    for i, v in enumerate((norm_mix_g, pool_scale, norm_mlp_g)):
        scl[:, i * 8:(i + 1) * 8] = np.asarray(v, f)[0].reshape(8, 128).T
    shared["scl"] = scl
    shared["gfin"] = np.asarray(norm_final_g, f).reshape(1, D)

    cst = np.zeros((128, 1216), f)
    cst[:, 0:128] = np.eye(128, dtype=f)
    pm = np.zeros((128, 4, 2, 128), f)
    tp = np.arange(128)[:, None]
    tq = np.arange(128)[None, :]
    for g, wdw in enumerate((2, 4, 8, 16)):
        d = tq - tp
        pm[:, g, 0, :] = np.where((d >= 0) & (d < wdw), 1.0 / wdw, 0.0) - (d == 0)
        d2 = tq + 128 - tp
        pm[:, g, 1, :] = np.where((d2 >= 0) & (d2 < wdw), 1.0 / wdw, 0.0)
    cst[:, 128:1152] = pm.reshape(128, 1024)
    cst[:, 1152:1216] = 1.0
    shared["cst"] = cst

    rbx = np.concatenate([rb, np.full((1, 16), NEG, f)], 0)
    sk = np.asarray(sinks, f)[0]
    kk = np.arange(128)[:, None]
    qq = np.arange(128)[None, :]
    d_cur = qq - kk
    idx_cur = np.where(d_cur >= 0, _bucket(d_cur), 32)
    d_prev = qq + 128 - kk
    idx_prev = np.where(d_prev < 128, _bucket(d_prev), 32)
    idx_masked = np.full((128, 128), 32, np.int64)
    idx_meta_std = np.full((128, 128), 32, np.int64)
    idx_meta_std[0:16, :] = 31
    idx_meta_first = np.full((128, 128), 32, np.int64)
    idx_meta_first[0:16, :] = _bucket(qq + 16 - kk[0:16])

    def expand(idx, kvh, sink_row):
        t = np.empty((128, 512), f)
        for g in range(4):
            t[:, g * 128:(g + 1) * 128] = rbx[idx, kvh * 4 + g]
            if sink_row:
                t[16, g * 128:(g + 1) * 128] = sk[kvh * 4 + g]
        return t

    tab_cur = np.stack([expand(idx_cur, kvh, False) for kvh in range(4)], 1)
    shared["tab_cur"] = np.ascontiguousarray(tab_cur)
    std = [np.concatenate([expand(idx_meta_std, kvh, True), expand(idx_prev, kvh, False)], 1) for kvh in range(4)]
    fst = [np.concatenate([expand(idx_meta_first, kvh, True), expand(idx_masked, kvh, False)], 1) for kvh in range(4)]
    tab_std = np.stack(std + std, 1)
    tab_even = np.stack(fst + std, 1)

    in_maps = []
    for c in range(NCORES):
        bi, half = c // 2, c % 2
        m = dict(shared)
        m["x_own"] = np.ascontiguousarray(x[bi, half * TOK:(half + 1) * TOK])
        xp = np.zeros((128 + NMETA, D), f)
        if half == 0:
            xp[112:128] = meta
        else:
            xp[0:128] = x[bi, TOK - 128:TOK]
        xp[128:] = meta
        m["x_pro"] = xp
        m["tab_mp"] = np.ascontiguousarray(tab_even if half == 0 else tab_std)
        in_maps.append(m)
    return in_maps


_CACHE = {}


def kernel(**inputs):
    in_maps = prepare_inputs(**inputs)
    if "nc" not in _CACHE:
        _CACHE["nc"] = build_program()[0]
    nc = _CACHE["nc"]
    res = run_bass_kernel_spmd(nc, in_maps, core_ids=list(range(NCORES)))
    out = np.empty((4, 2 * TOK, D), np.float32)
    for c in range(NCORES):
        out[c // 2, (c % 2) * TOK:(c % 2 + 1) * TOK] = res.results[c]["out"]
    return out
```
